# Optimizing a Trainium2 kernel written in Bass

```python
import functools
import jax, jax.numpy as jnp
from jax import lax
import numpy as np

D_MODEL = 4096
BATCH = 4
SEQ = 2048
DEPTH = 1
DEC_BATCH = 32
DEC_SEQ = 4
PAST_LEN = 8192
PAGE_SIZE = 128

D_CONV = D_MODEL // 2
CONV_W = 3
N_HEADS = 16
HEAD_DIM = 128
N_KV_HEADS = 4
KV_GROUP = N_HEADS // N_KV_HEADS
D_ATTN = N_HEADS * HEAD_DIM
N_IDX_HEADS = 16
IDX_DIM = 64
TOPK_MAX = 256
D_FF = 4 * D_MODEL
ROPE_THETA = 10000.0
NORM_EPS = 1e-6
Q_BLOCK = 128
N_MOD = 6
COL_SIZES = (D_CONV, D_CONV, D_CONV,
             D_ATTN, N_KV_HEADS * HEAD_DIM, N_KV_HEADS * HEAD_DIM,
             N_IDX_HEADS * IDX_DIM, IDX_DIM, N_IDX_HEADS,
             D_MODEL, D_MODEL)
D_IN = 3 * D_CONV + D_ATTN + 2 * N_KV_HEADS * HEAD_DIM + N_IDX_HEADS * IDX_DIM + IDX_DIM + N_IDX_HEADS + 2 * D_MODEL

kernel_name = "hybrid_conv_dsa_adaln_decode_step"


def rmsnorm(x, g):
    xf = x.astype(jnp.float32)
    y = xf * lax.rsqrt(jnp.mean(xf * xf, axis=-1, keepdims=True) + NORM_EPS)
    return (y * g.astype(jnp.float32)).astype(x.dtype)


def rope(x, pos):
    half = x.shape[-1] // 2
    inv = ROPE_THETA ** (-jnp.arange(half, dtype=jnp.float32) / half)
    ang = pos.astype(jnp.float32)[:, None] * inv[None, :]
    cos = jnp.cos(ang)[:, None, :]
    sin = jnp.sin(ang)[:, None, :]
    x1 = x[..., :half].astype(jnp.float32)
    x2 = x[..., half:].astype(jnp.float32)
    out = jnp.concatenate([x1 * cos - x2 * sin, x2 * cos + x1 * sin], axis=-1)
    return out.astype(x.dtype)


def split_cols(z):
    offs = []
    acc = 0
    for s in COL_SIZES[:-1]:
        acc += s
        offs.append(acc)
    return jnp.split(z, offs, axis=-1)


def modulations(c, w_ada, b_ada):
    m = jax.nn.silu(c) @ w_ada + b_ada
    return jnp.split(m[:, None, :], N_MOD, axis=-1)


def short_conv(u, buf, conv_w):
    t = u.shape[1]
    ext = jnp.concatenate([buf, u], axis=1)
    y = ext[:, 0:t] * conv_w[0]
    for j in range(1, CONV_W):
        y = y + ext[:, j:j + t] * conv_w[j]
    return y, ext[:, -(CONV_W - 1):]


def gather_rows(a, idx):
    return jax.vmap(lambda ab, ib: ab[ib])(a, idx)


def indexer_select(qi, wi, qpos, ki, k_top):
    s = jnp.einsum('bthd,bsd->bths', qi, ki, preferred_element_type=jnp.float32) * (IDX_DIM ** -0.5)
    score = jnp.einsum('bth,bths->bts', wi.astype(jnp.float32), jax.nn.relu(s))
    kpos = jnp.arange(ki.shape[1])
    admissible = kpos[None, None, :] <= qpos[None, :, None]
    score = jnp.where(admissible, score, -jnp.inf)
    _, idx = lax.top_k(score, k_top)
    valid = idx <= qpos[None, :, None]
    return idx, valid


def sparse_attend(q, k_sel, v_sel, valid):
    b, t = q.shape[:2]
    qg = q.reshape(b, t, N_KV_HEADS, KV_GROUP, HEAD_DIM)
    s = jnp.einsum('btgrd,btkgd->btgrk', qg, k_sel, preferred_element_type=jnp.float32) * (HEAD_DIM ** -0.5)
    s = jnp.where(valid[:, :, None, None, :], s, -jnp.inf)
    p = jax.nn.softmax(s, axis=-1)
    o = jnp.einsum('btgrk,btkgd->btgrd', p.astype(v_sel.dtype), v_sel)
    return o.reshape(b, t, D_ATTN)


def prompt_attention(q, k, v, qi, ki, wi):
    b, s_len = q.shape[:2]
    k_top = min(TOPK_MAX, s_len // 4)
    nb = s_len // Q_BLOCK

    def to_blocks(a):
        return jnp.swapaxes(a.reshape((b, nb, Q_BLOCK) + a.shape[2:]), 0, 1)

    def one_block(args):
        qb, qib, wib, qpos = args
        idx, valid = indexer_select(qib, wib, qpos, ki, k_top)
        return sparse_attend(qb, gather_rows(k, idx), gather_rows(v, idx), valid)

    pos = jnp.arange(s_len).reshape(nb, Q_BLOCK)
    out = lax.map(one_block, (to_blocks(q), to_blocks(qi), to_blocks(wi), pos))
    return jnp.swapaxes(out, 0, 1).reshape(b, s_len, D_ATTN)


def sample_attention(q, k, v, qi, ki, wi, cache_k, cache_v, cache_kidx, page_table):
    b, t = q.shape[:2]
    page = cache_k.shape[1]
    past = page_table.shape[1] * page
    k_top = min(TOPK_MAX, (past + t) // 4)
    ki_past = cache_kidx[page_table].reshape(b, past, IDX_DIM)
    ki_all = jnp.concatenate([ki_past, ki], axis=1)
    qpos = past + jnp.arange(t)
    idx, valid = indexer_select(qi, wi, qpos, ki_all, k_top)
    in_past = idx < past
    pidx = jnp.minimum(idx, past - 1)
    phys = jnp.take_along_axis(page_table, (pidx // page).reshape(b, -1), axis=1).reshape(idx.shape)
    off = pidx % page
    nidx = jnp.clip(idx - past, 0, t - 1)
    sel = in_past[..., None, None]
    k_sel = jnp.where(sel, cache_k[phys, off], gather_rows(k, nidx))
    v_sel = jnp.where(sel, cache_v[phys, off], gather_rows(v, nidx))
    return sparse_attend(q, k_sel, v_sel, valid)


def layer_block(x, c, pos, conv_buf, attention, w_ada, b_ada, g_mix, w_in, conv_w,
                w_conv_out, w_attn_out, w_out, g_ffn, w_up, w_down):
    b, t, _ = x.shape
    sh1, sc1, gt1, sh2, sc2, gt2 = modulations(c, w_ada, b_ada)
    h = rmsnorm(x, g_mix) * (1 + sc1) + sh1
    z = h @ w_in
    xin, bg, cg, q, k, v, qi, ki, wi, g_c, g_a = split_cols(z)
    u_conv, conv_buf_new = short_conv(cg * xin, conv_buf, conv_w)
    y_conv = (bg * u_conv) @ w_conv_out
    q = rope(q.reshape(b, t, N_HEADS, HEAD_DIM), pos)
    k = rope(k.reshape(b, t, N_KV_HEADS, HEAD_DIM), pos)
    v = v.reshape(b, t, N_KV_HEADS, HEAD_DIM)
    qi = rope(qi.reshape(b, t, N_IDX_HEADS, IDX_DIM), pos)
    ki = rope(ki[:, :, None, :], pos)[:, :, 0, :]
    wi = wi * (N_IDX_HEADS ** -0.5)
    y_attn = attention(q, k, v, qi, ki, wi) @ w_attn_out
    merged = jax.nn.sigmoid(g_c) * y_conv + jax.nn.sigmoid(g_a) * y_attn
    x = x + gt1 * (merged @ w_out)
    h2 = rmsnorm(x, g_ffn) * (1 + sc2) + sh2
    x = x + gt2 * (jnp.square(jax.nn.relu(h2 @ w_up)) @ w_down)
    return x, conv_buf_new, k, v, ki


def setup_inputs(seed: int = 0) -> dict:
    key = jax.random.key(seed)
    ks = jax.random.split(key, 24)
    f32 = jnp.float32
    n_pages = PAST_LEN // PAGE_SIZE
    n_used = DEC_BATCH * n_pages
    n_pool = n_used + (n_used + 3) // 4

    def nrm(k, shape, scale):
        return jax.random.normal(k, shape, f32) * scale

    page_table = jax.random.permutation(ks[0], n_pool)[:n_used].reshape(DEC_BATCH, n_pages).astype(jnp.int32)
    return {
        "x_prompt": nrm(ks[1], (BATCH, SEQ, D_MODEL), 1.0),
        "x_sample": nrm(ks[2], (DEC_BATCH, DEC_SEQ, D_MODEL), 1.0),
        "cache_k": nrm(ks[3], (DEPTH, n_pool, PAGE_SIZE, N_KV_HEADS, HEAD_DIM), 1.0),
        "cache_v": nrm(ks[4], (DEPTH, n_pool, PAGE_SIZE, N_KV_HEADS, HEAD_DIM), 1.0),
        "cache_kidx": nrm(ks[5], (DEPTH, n_pool, PAGE_SIZE, IDX_DIM), 1.0),
        "state_conv": nrm(ks[6], (DEPTH, DEC_BATCH, CONV_W - 1, D_CONV), 1.0),
        "page_table": page_table,
        "c_prompt": nrm(ks[7], (BATCH, D_MODEL), 1.0),
        "c_sample": nrm(ks[8], (DEC_BATCH, D_MODEL), 1.0),
        "w_ada": nrm(ks[9], (DEPTH, D_MODEL, N_MOD * D_MODEL), 0.5 * D_MODEL ** -0.5),
        "b_ada": nrm(ks[10], (DEPTH, N_MOD * D_MODEL), 0.01),
        "g_mix": 1.0 + nrm(ks[11], (DEPTH, D_MODEL), 0.02),
        "w_in": nrm(ks[12], (DEPTH, D_MODEL, D_IN), D_MODEL ** -0.5),
        "conv_w": nrm(ks[13], (DEPTH, CONV_W, D_CONV), CONV_W ** -0.5),
        "w_conv_out": nrm(ks[14], (DEPTH, D_CONV, D_MODEL), D_CONV ** -0.5),
        "w_attn_out": nrm(ks[15], (DEPTH, D_ATTN, D_MODEL), D_ATTN ** -0.5),
        "w_out": nrm(ks[16], (DEPTH, D_MODEL, D_MODEL), D_MODEL ** -0.5),
        "g_ffn": 1.0 + nrm(ks[17], (DEPTH, D_MODEL), 0.02),
        "w_up": nrm(ks[18], (DEPTH, D_MODEL, D_FF), D_MODEL ** -0.5),
        "w_down": nrm(ks[19], (DEPTH, D_FF, D_MODEL), D_FF ** -0.5),
        "g_final": 1.0 + nrm(ks[20], (D_MODEL,), 0.02),
    }


def reference(x_prompt, x_sample, cache_k, cache_v, cache_kidx, state_conv, page_table,
              c_prompt, c_sample, w_ada, b_ada, g_mix, w_in, conv_w, w_conv_out, w_attn_out,
              w_out, g_ffn, w_up, w_down, g_final):
    b_p, s_p, _ = x_prompt.shape
    t_s = x_sample.shape[1]
    past = page_table.shape[1] * cache_k.shape[2]
    pos_p = jnp.arange(s_p)
    pos_s = past + jnp.arange(t_s)
    conv_zero = jnp.zeros((b_p, CONV_W - 1, D_CONV), x_prompt.dtype)
    xp, xs = x_prompt, x_sample
    kp_l, vp_l, kip_l, cp_l = [], [], [], []
    ks_l, vs_l, kis_l, cs_l = [], [], [], []
    for l in range(DEPTH):
        wl = (w_ada[l], b_ada[l], g_mix[l], w_in[l], conv_w[l], w_conv_out[l],
              w_attn_out[l], w_out[l], g_ffn[l], w_up[l], w_down[l])
        xp, cbp, kp, vp, kip = layer_block(xp, c_prompt, pos_p, conv_zero, prompt_attention, *wl)
        samp_attn = functools.partial(sample_attention, cache_k=cache_k[l], cache_v=cache_v[l],
                                      cache_kidx=cache_kidx[l], page_table=page_table)
        xs, cbs, ksn, vsn, kis = layer_block(xs, c_sample, pos_s, state_conv[l], samp_attn, *wl)
        kp_l.append(kp); vp_l.append(vp); kip_l.append(kip); cp_l.append(cbp)
        ks_l.append(ksn); vs_l.append(vsn); kis_l.append(kis); cs_l.append(cbs)
    y_prompt = rmsnorm(xp, g_final)
    y_sample = rmsnorm(xs, g_final)
    k_prompt = jnp.stack(kp_l)
    v_prompt = jnp.stack(vp_l)
    kidx_prompt = jnp.stack(kip_l)
    conv_prompt = jnp.stack(cp_l)
    k_sample = jnp.stack(ks_l)
    v_sample = jnp.stack(vs_l)
    kidx_sample = jnp.stack(kis_l)
    conv_sample = jnp.stack(cs_l)
    return (y_prompt, y_sample, k_prompt, v_prompt, kidx_prompt, conv_prompt,
            k_sample, v_sample, kidx_sample, conv_sample)
```

```python
import contextlib
import numpy as np
import concourse.bass as bass
import concourse.mybir as mybir
from concourse.bass_utils import run_bass_kernel_spmd

F32 = mybir.dt.float32
BF16 = mybir.dt.bfloat16
I32 = mybir.dt.int32
ALU = mybir.AluOpType
AF = mybir.ActivationFunctionType
AX = mybir.AxisListType

D = 4096
KC = 32
NG = 16
GPS = 8
NSRC = 9
NCORES = 2
GP = 256
SC = 256
HC = 260
G = 262
DCONV = 2048
DFF = 16384
PAST = 8192
NPAGE = 64
EPS = 1e-6
OFF = dict(xin=0, bg=2048, cg=4096, q=6144, k=8192, v=8704, qi=9216, ki=10240, wi=10304,
           gc=10320, ga=14416)
NBIS = 34
BIS_LO = -1024.0
NEG = -1.0e30
MBIAS = -30000.0
NS_RING = 2
FFB = 4

import os
STAGE_LIMIT = int(os.environ.get("MK_STAGE", "99"))


class _Stop(Exception):
    pass


def CHECK(level):
    if STAGE_LIMIT < level:
        raise _Stop()


def weight_slots():
    slots = []
    allk = list(range(KC))
    for cc in range(16):
        for nm in ("xin", "bg", "cg"):
            slots.append((f"A_{nm}_{cc}", "in", allk, [(OFF[nm] + cc * 128, 128)]))
    for j in range(32):
        slots.append((f"B_gc_{j}", "in", allk, [(OFF["gc"] + j * 128, 128)]))
        slots.append((f"B_co_{j}", "co", list(range(16)), [(j * 128, 128)]))
    for h in range(16):
        slots.append((f"C_q_{h}", "in", allk, [(OFF["q"] + h * 128, 128)]))
    for g in range(4):
        slots.append((f"C_k_{g}", "in", allk, [(OFF["k"] + g * 128, 128)]))
    for g in range(4):
        slots.append((f"C_v_{g}", "in", allk, [(OFF["v"] + g * 128, 128)]))
    for b in range(8):
        slots.append((f"C_qi_{b}", "in", allk, [(OFF["qi"] + b * 128, 128)]))
    slots.append(("C_ki", "in", allk, [(OFF["ki"], 64), (OFF["ki"], 64)]))
    slots.append(("C_wi", "in", allk, [(OFF["wi"], 16)]))
    for j in range(32):
        slots.append((f"D_ga_{j}", "in", allk, [(OFF["ga"] + j * 128, 128)]))
        slots.append((f"D_ao_{j}", "ao", list(range(16)), [(j * 128, 128)]))
    for j in range(32):
        slots.append((f"E_{j}", "out", allk, [(j * 128, 128)]))
    for fb in range(FFB):
        for f in range(32):
            slots.append((f"F_up_{fb}_{f}", "up", allk, [((fb * 32 + f) * 128, 128)]))
        for j in range(32):
            slots.append((f"F_dn_{fb}_{j}", "down", list(range(fb * 32, fb * 32 + 32)), [(j * 128, 128)]))
    return slots


def slot_layout():
    off = 0
    lay = {}
    for name, mat, kcs, crs in weight_slots():
        width = sum(w for _, w in crs)
        n = len(kcs)
        lay[name] = (off, n, width)
        off += 128 * n * width
    return lay, off


def build_wflat(mats):
    lay, total = slot_layout()
    flat = np.empty(total, np.float32)
    for name, mat, kcs, crs in weight_slots():
        off, n, width = lay[name]
        W = mats[mat]
        r0 = kcs[0] * 128
        rows = W[r0:r0 + n * 128]
        blk = np.concatenate([rows[:, c0:c0 + w] for c0, w in crs], axis=1) if len(crs) > 1 else rows[:, crs[0][0]:crs[0][0] + crs[0][1]]
        flat[off:off + 128 * n * width] = blk.reshape(n, 128, width).transpose(1, 0, 2).reshape(-1)
    return flat


def build_adaflat(w_ada):
    W = w_ada.reshape(KC, 128, 192, 128)
    return np.ascontiguousarray(W.transpose(2, 1, 0, 3)).reshape(-1)


ENGS = ("pe", "act", "dve", "pool", "sp")
EPOCH = 4000


class Sched:
    def __init__(self):
        self.ops = {e: [] for e in ENGS}
        self.cnt = {e: 0 for e in ENGS}
        self.res_w = {}
        self.res_r = {}
        self.seen = {e: {} for e in ENGS}
        self.semnames = set()
        self.dma_pool = {"sp": [f"dsp{i}" for i in range(16)], "pool": [f"dpl{i}" for i in range(16)]}
        self.dma_i = {"sp": 0, "pool": 0}
        self.dma_val = {}
        self.sems = {}

    def _waits(self, eng, r, w):
        toks = []
        for res in r:
            t = self.res_w.get(res)
            if t is not None:
                toks.append(t)
        for res in w:
            t = self.res_w.get(res)
            if t is not None:
                toks.append(t)
            for t in self.res_r.get(res, {}).values():
                toks.append(t)
        out = {}
        for sem, val, src in toks:
            if src == "pe" and eng == "pe":
                continue
            if self.seen[eng].get(sem, 0) >= val:
                continue
            if out.get(sem, 0) < val:
                out[sem] = val
        for sem, val in out.items():
            self.seen[eng][sem] = val
        return list(out.items())

    def _commit(self, tok, r, w):
        for res in w:
            self.res_w[res] = tok
            self.res_r[res] = {}
        for res in r:
            self.res_r.setdefault(res, {})[tok[0]] = tok

    def op(self, eng, fn, r=(), w=(), inc=True):
        waits = self._waits(eng, r, w)
        if inc:
            self.cnt[eng] += 1
            c = self.cnt[eng]
            sem = f"c_{eng}_{(c - 1) // EPOCH}"
            val = (c - 1) % EPOCH + 1
            tok = (sem, val, eng)
        else:
            c = self.cnt[eng] + 1
            sem = f"c_{eng}_{(c - 1) // EPOCH}"
            val = (c - 1) % EPOCH + 1
            tok = (sem, val, eng)
        self.semnames.add(sem)
        for s, _ in waits:
            self.semnames.add(s)
        self.ops[eng].append((waits, fn, (sem, 1) if inc else None))
        self._commit(tok, r, w)

    def dma(self, q, fn, r=(), w=()):
        pool = self.dma_pool[q]
        sem = pool[self.dma_i[q] % len(pool)]
        self.dma_i[q] += 1
        waits = self._waits(q, r, w)
        prev = self.dma_val.get(sem, 0)
        if prev and self.seen[q].get(sem, 0) < prev:
            waits.append((sem, prev))
            self.seen[q][sem] = prev
        val = prev + 16
        self.dma_val[sem] = val
        self.semnames.add(sem)
        for s, _ in waits:
            self.semnames.add(s)
        self.ops[q].append((waits, fn, (sem, 16)))
        self._commit((sem, val, "dma_" + q), r, w)

    def final_wait_all(self, eng):
        waits = []
        for sem, val in self.dma_val.items():
            waits.append((sem, val))
        for e in ENGS:
            c = self.cnt[e]
            if c:
                waits.append((f"c_{e}_{(c - 1) // EPOCH}", (c - 1) % EPOCH + 1))
        self.ops[eng].append((waits, None, None))

    def emit(self, nc, stack):
        for name in sorted(self.semnames):
            self.sems[name] = stack.enter_context(nc.semaphore(name))
        sems = self.sems
        ops = self.ops

        def run(engobj, lst):
            for waits, fn, inc in lst:
                for s, v in waits:
                    engobj.wait_ge(sems[s], v)
                if fn is None:
                    continue
                ins = fn(engobj)
                if inc is not None:
                    ins.then_inc(sems[inc[0]], inc[1])

        with nc.Block() as block:
            @block.tensor
            def _(e):
                run(e, ops["pe"])

            @block.scalar
            def _(e):
                run(e, ops["act"])

            @block.vector
            def _(e):
                run(e, ops["dve"])

            @block.gpsimd
            def _(e):
                run(e, ops["pool"])

            @block.sync
            def _(e):
                run(e, ops["sp"])


def build_program():
    nc = bass.Bass("TRN2", target_bir_lowering=False)
    S = Sched()
    lay, wtotal = slot_layout()
    TINY = os.environ.get("MK_TINY", "0") == "1"
    NADA = 192
    NPOOLROWS = 2560 * 128
    if TINY:
        lay = {k: (0, n, w) for k, (o, n, w) in lay.items()}
        wtotal = 128 * 32 * 128
        NADA = 1
        NPOOLROWS = 16 * 128

    def din(name, shape, dt=F32):
        return nc.dram_tensor(name, list(shape), dt, kind="ExternalInput").ap()

    def dout(name, shape, dt=F32):
        return nc.dram_tensor(name, list(shape), dt, kind="ExternalOutput").ap()

    x_own = din("x_own", [4096, D])
    x_small = din("x_small", [NG, 6, D])
    c5 = din("c5", [2, NSRC, D])
    flags = din("flags", [128, 2])
    cs_own = din("cs_own", [2048, 2, 64])
    cs_small = din("cs_small", [6, 2, 64])
    csi_own = din("csi_own", [2048, 2, 64])
    csi_small = din("csi_small", [6, 2, 64])
    ident_in = din("ident", [128, 128])
    tri_in = din("tri", [128, 128])
    wflat = din("wflat", [wtotal])
    adaflat = din("adaflat", [NADA * 128 * KC * 128])
    b_ada = din("b_ada", [192, 128])
    vecs = din("vecs", [96, 128])
    convw = din("convw", [48, 128])
    sconv = din("sconv", [16, 32, 128])
    cache_k = din("cache_k", [NPOOLROWS, 512])
    cache_v = din("cache_v", [NPOOLROWS, 512])
    cache_ki = din("cache_ki", [NPOOLROWS, 64])
    ptab = din("ptab", [16, NPAGE], I32)
    pidx_in = din("pidx", [128, 1])
    triT_in = din("triT", [128, 128])
    y_own = dout("y_own", [4096, D])
    y_s = dout("y_s", [64, D])
    k_own = dout("k_own", [4, 4096, 128])
    v_own = dout("v_own", [4, 4096, 128])
    ki_own = dout("ki_own", [4096, 64])
    k_s = dout("k_s", [4, 64, 128])
    v_s = dout("v_s", [4, 64, 128])
    ki_s = dout("ki_s", [64, 64])
    conv_p = dout("conv_p", [2, 32, 128])
    conv_s = dout("conv_s", [16, 32, 128])

    scr_ki = nc.dram_tensor("scr_ki", [16, NPAGE, 128, 64], BF16, kind="Internal").ap()
    scr_k = nc.dram_tensor("scr_k", [16, NPAGE, 128, 512], BF16, kind="Internal").ap()
    scr_v = nc.dram_tensor("scr_v", [16, NPAGE, 128, 512], BF16, kind="Internal").ap()

    stack = contextlib.ExitStack()
    with stack:
        def sb(name, shape, dt=F32):
            return stack.enter_context(nc.sbuf_tensor("s_" + name, list(shape), dt))

        kip = sb("kip", [128, 2, 128], BF16)
        xT = sb("xT", [128, KC, G])
        hT = sb("hT", [128, KC, G], BF16)
        mT = sb("mT", [128, KC, G], BF16)
        qv = sb("qv", [128, 32, G], BF16)
        vcT = qv[:, 0:16, :]
        qT = qv[:, 16:32, :]
        actT = [qv]
        kT = sb("kT", [128, 4, 2048], BF16)
        v_sb = sb("v_sb", [128, 16, 512], BF16)
        kiT2 = sb("kiT2", [128, 2048], BF16)
        qiT = sb("qiT", [128, 8, G], BF16)
        ksT = sb("ksT", [128, 4, 4], BF16)
        vs_sb = sb("vs_sb", [4, 512], BF16)
        kisT2 = sb("kisT2", [128, 4], BF16)
        wring = sb("wring", [128, NS_RING, 32 * 128], BF16)
        xst = [sb(f"xst{i}", [128, 1024]) for i in range(2)]
        small = xst[1]
        sc = sb("sc", [128, 2048])
        mb = sb("mb", [128, 2048], BF16)
        rtmp2 = [sb(f"rtmp{i}", [128, 512]) for i in range(2)]
        pT = [sb(f"pT{i}", [128, 512], BF16) for i in range(2)]
        tmpA = [sb(f"tmpA{i}", [128, G]) for i in range(2)]
        tmpB = [sb(f"tmpB{i}", [128, G]) for i in range(2)]
        rstd = sb("rstd", [128, G])
        ident = sb("ident", [128, 128])
        identb = sb("identb", [128, 128], BF16)
        ident4 = sb("ident4", [128, 4, 128], BF16)
        tri = sb("tri", [128, 128])
        ones_f = sb("ones_f", [128, 128])
        ones_b = sb("ones_b", [128, 128], BF16)
        flg = sb("flg", [128, 2])
        epsb = sb("epsb", [128, 1])
        modT = sb("modT", [128, 6, NSRC, KC])
        A1 = sb("A1", [128, NSRC, KC])
        A2 = sb("A2", [128, NSRC, KC])
        badaT = sb("badaT", [128, 192])
        vecT = sb("vecT", [128, 96])
        cwT = sb("cwT", [128, 48])
        scvT = sb("scvT", [128, 1, 32])
        sT = sb("sT", [128, KC, NSRC], BF16)
        aext = sb("aext", [128, 2 + GP])
        aexs = sb("aexs", [128, 8])
        ubuf = sb("ubuf", [128, GP])
        ubs = sb("ubs", [128, 4])
        alast_p = sb("alast_p", [128, 32])
        alast_s = sb("alast_s", [128, 32])
        cs_t = sb("cs_t", [128, 3, 2, 64])
        csi_t = sb("csi_t", [128, 3, 2, 64])
        rp = [sb(f"rp{i}", [128, 64]) for i in range(4)]
        tokb = sb("tokb", [128, 128], BF16)
        kst = sb("kst", [128, 2, 128])
        vst = sb("vst", [128, 2, 128])
        kist = sb("kist", [128, 3, 64])
        w_tok = sb("w_tok", [128, 3, 16])
        bis = sb("bis", [128, 8])
        kipT = sb("kipT", [128, 512], BF16)
        scT = sb("scT", [128, 4, 65])
        wbc8 = sb("wbc8", [128, 8, 64])
        wd = sb("wd", [4, 64])
        mbT4 = sb("mbT4", [128, 65, 16], BF16)
        kpg = sb("kpg", [128, 2, 512], BF16)
        vpg = sb("vpg", [128, 2, 512], BF16)
        kpT = sb("kpT", [128, 1, 512], BF16)
        pTs = sb("pTs", [128, 2, 64], BF16)
        pti = sb("pti", [128, NPAGE], I32)
        idxi = sb("idxi", [128, NPAGE], I32)
        pidx = sb("pidx", [128, 1])
        bisS = sb("bisS", [128, 16])
        red1 = sb("red1", [128, 32])
        zeros_b = sb("zeros_b", [128, 128], BF16)
        triT = sb("triT", [4, 4])
        ost = sb("ost", [32, 128])

        ps = [stack.enter_context(nc.psum_tensor(f"ps{i}", [128, 512], F32)) for i in range(8)]

        def P(i):
            return ("ps", i)

        def psb(i):
            return ps[i][:, :].bitcast(BF16)

        def dve(fn, r=(), w=()):
            S.op("dve", fn, r, w)

        def act(fn, r=(), w=()):
            S.op("act", fn, r, w)

        def pe(fn, r=(), w=(), inc=True):
            S.op("pe", fn, r, w, inc)

        def dma_sp(out, in_, r=(), w=()):
            S.dma("sp", lambda e, o=out, i=in_: e.dma_start(out=o, in_=i), r, w)

        def dma_out(out, in_, r=(), w=()):
            if os.environ.get("MK_OQ", "sp") == "pool":
                dma_pl(out, in_, r, w)
            else:
                dma_sp(out, in_, r, w)

        def dma_pl(out, in_, r=(), w=()):
            S.dma("pool", lambda e, o=out, i=in_: e.dma_start(out=o, in_=i), r, w)

        ring_i = [0]

        def load_slot(name):
            off, n, width = lay[name]
            if os.environ.get("MK_DBG") == "n32":
                n = 32
            s = ring_i[0] % NS_RING
            ring_i[0] += 1
            dst = wring[:, s, 0:n * width]
            src = wflat[off:off + 128 * n * width].rearrange("(p f) -> p f", p=128)
            S.dma("pool", lambda e, o=dst, i=src: e.dma_start(out=o, in_=i, max_dma_last_dim=4096),
                  r=(), w=(("wr", s),))
            return s, wring[:, s, 0:n * width].rearrange("p (u c) -> p u c", c=width), n

        try:
            dma_sp(ident[:, :], ident_in, w=("ident",))
            dma_sp(tri[:, :], tri_in, w=("tri",))
            dma_sp(flg[:, :], flags, w=("flg",))
            dve(lambda e: e.tensor_copy(out=identb[:, :], in_=ident[:, :]), r=("ident",), w=("identb",))
            for i in range(4):
                dve(lambda e, i=i: e.tensor_copy(out=ident4[:, i, :], in_=ident[:, :]), r=("ident",), w=("ident4",))
            dve(lambda e: e.memset(ones_f[:, :], 1.0), w=("ones_f",))
            dve(lambda e: e.memset(ones_b[:, :], 1.0), w=("ones_b",))
            dve(lambda e: e.memset(epsb[:, :], EPS), w=("epsb",))
            dve(lambda e: e.memset(qv[:, :, :], 0.0), w=("vcT", "qT", "actT"))
            dve(lambda e: e.memset(zeros_b[:, :], 0.0), w=("zeros_b",))
            dma_sp(triT[:, :], triT_in[0:4, 0:4], w=("triT",))
            dma_sp(pidx[:, :], pidx_in, w=("pidx",))

            def transpose_rows(src_ap, rows, dst_ap, bank, r, w, use_act=False):
                pe(lambda e, s=src_ap, b=bank, rows=rows: e.transpose(out=ps[b][:, 0:rows], in_=s, identity=ident[0:rows, 0:rows]),
                   r=tuple(r) + ("ident",), w=(P(bank),))
                if use_act:
                    act(lambda e, b=bank, rows=rows, d=dst_ap: e.copy(out=d, in_=ps[b][:, 0:rows]), r=(P(bank),), w=w)
                else:
                    dve(lambda e, b=bank, rows=rows, d=dst_ap: e.tensor_copy(out=d, in_=ps[b][:, 0:rows]), r=(P(bank),), w=w)

            if os.environ.get("MK_NOPRO") != "1":
                for sq in range(16):
                    dma_sp(pti[:, :], ptab[sq].partition_broadcast(128), w=("pti",))
                    dve(lambda e: e.tensor_scalar(out=idxi[:, :], in0=pti[:, :], scalar1=128.0, scalar2=pidx[:, 0:1], op0=ALU.mult, op1=ALU.add), r=("pti", "pidx"), w=("idxi",))
                    for j in range(NPAGE):
                        kb = j % 2
                        for (cache, buf, bn, width, scr) in ((cache_ki, kip, "kip", 64, scr_ki), (cache_k, kpg, "kpg", 512, scr_k), (cache_v, vpg, "vpg", 512, scr_v)):
                            S.dma("pool", lambda e, j=j, kb=kb, cache=cache, buf=buf, width=width: e.indirect_dma_start(
                                out=buf[:, kb, 0:width], out_offset=None, in_=cache, in_offset=bass.IndirectOffsetOnAxis(ap=idxi[:, j:j + 1], axis=0)),
                                  r=("idxi",), w=((bn, kb),))
                            dma_sp(scr[sq, j], buf[:, kb, 0:width], r=((bn, kb),), w=(("scr", sq),))
            CHECK(1)
            dma_sp(small[0:96, 0:128], vecs, w=(("xst", 1),))
            transpose_rows(small[0:96, 0:128], 96, vecT[:, :], 0, (("xst", 1),), ("vecT",))
            dma_sp(small[0:48, 0:128], convw, r=(), w=(("xst", 1),))
            transpose_rows(small[0:48, 0:128], 48, cwT[:, :], 1, (("xst", 1),), ("cwT",))
            for hh in range(2):
                dma_sp(small[0:96, 0:128], b_ada[hh * 96:(hh + 1) * 96, :], w=(("xst", 1),))
                transpose_rows(small[0:96, 0:128], 96, badaT[:, hh * 96:(hh + 1) * 96], hh, (("xst", 1),), ("badaT",))
            CHECK(2)
            def adaln(ph):
                for qq in range(4):
                    dma_sp(small[0:NSRC, :], c5[ph][:, qq * 1024:(qq + 1) * 1024], w=(("xst", 1),))
                    act(lambda e: e.activation(out=small[0:NSRC, :], in_=small[0:NSRC, :], func=AF.Silu), r=(("xst", 1),), w=(("xst", 1),))
                    for k8 in range(8):
                        kc = qq * 8 + k8
                        b = kc % 2
                        pe(lambda e, k8=k8, b=b: e.transpose(out=ps[b][:, 0:NSRC], in_=small[0:NSRC, k8 * 128:(k8 + 1) * 128], identity=ident[0:NSRC, 0:NSRC]),
                           r=(("xst", 1), "ident"), w=(P(b),))
                        dve(lambda e, kc=kc, b=b: e.tensor_copy(out=sT[:, kc, :], in_=ps[b][:, 0:NSRC]), r=(P(b),), w=("sT",))
                for fc in range(192):
                    s = ring_i[0] % NS_RING
                    ring_i[0] += 1
                    fo = (fc % NADA) * 128 * KC * 128
                    src = adaflat[fo:fo + 128 * KC * 128].rearrange("(p f) -> p f", p=128)
                    S.dma("pool", lambda e, o=wring[:, s, :], i=src: e.dma_start(out=o, in_=i, max_dma_last_dim=4096),
                          w=(("wr", s),))
                    wv = wring[:, s, :].rearrange("p (u c) -> p u c", c=128)
                    b = 2 + fc % 2
                    for kc in range(KC):
                        pe(lambda e, kc=kc, b=b, wv=wv: e.matmul(ps[b][:, 0:NSRC], lhsT=wv[:, kc, :], rhs=sT[:, kc, :], start=(kc == 0), stop=(kc == KC - 1)),
                           r=(("wr", s), "sT"), w=(P(b),), inc=(kc == KC - 1))
                    m, kk = fc // KC, fc % KC
                    dve(lambda e, b=b, m=m, kk=kk, fc=fc: e.tensor_scalar(out=modT[:, m, :, kk], in0=ps[b][:, 0:NSRC], scalar1=badaT[:, fc:fc + 1], scalar2=None, op0=ALU.add),
                        r=(P(b), "badaT"), w=("modT",))
                for (Ax, mi, vo, nm) in ((A1, 1, 0, "A1"), (A2, 4, 32, "A2")):
                    for src_i in range(NSRC):
                        dve(lambda e, Ax=Ax, mi=mi, vo=vo, src_i=src_i: e.scalar_tensor_tensor(
                            out=Ax[:, src_i, :], in0=modT[:, mi, src_i, :], scalar=1.0, in1=vecT[:, vo:vo + KC], op0=ALU.add, op1=ALU.mult),
                            r=("modT", "vecT"), w=(nm,))


            def load_x_tiles(src_rows_fn, ncols_list):
                col = 0
                for t, rows in enumerate(ncols_list):
                    for qd in range(4):
                        st = xst[qd % 2]
                        stn = ("xst", qd % 2)
                        dma_sp(st[0:rows, :], src_rows_fn(t)[:, qd * 1024:(qd + 1) * 1024], w=(stn,))
                        for q2 in range(2):
                            b = q2
                            for i in range(4):
                                pe(lambda e, st=st, rows=rows, b=b, i=i, q2=q2: e.transpose(
                                    out=ps[b][:, i * 128:i * 128 + rows], in_=st[0:rows, (q2 * 4 + i) * 128:(q2 * 4 + i + 1) * 128], identity=ident[0:rows, 0:rows]),
                                   r=(stn, "ident"), w=(P(b),), inc=(i == 3))
                            kc0 = qd * 8 + q2 * 4
                            if q2 == 0:
                                dve(lambda e, b=b, kc0=kc0, col=col, rows=rows: e.tensor_copy(
                                    out=xT[:, kc0:kc0 + 4, col:col + rows], in_=ps[b][:, :].rearrange("p (i t) -> p i t", i=4)[:, :, 0:rows]),
                                    r=(P(b),), w=("xT",))
                            else:
                                act(lambda e, b=b, kc0=kc0, col=col, rows=rows: e.copy(
                                    out=xT[:, kc0:kc0 + 4, col:col + rows], in_=ps[b][:, :].rearrange("p (i t) -> p i t", i=4)[:, :, 0:rows]),
                                    r=(P(b),), w=("xT",))
                    col += rows

            def rms_rstd(ncols):
                for kc in range(KC):
                    t = tmpA[kc % 2]
                    tn = ("tmpA", kc % 2)
                    act(lambda e, kc=kc, t=t: e.activation(out=t[:, 0:ncols], in_=xT[:, kc, 0:ncols], func=AF.Square), r=("xT",), w=(tn,))
                    pe(lambda e, kc=kc, t=t: e.matmul(ps[7][:, 0:ncols], lhsT=ones_f[:, :], rhs=t[:, 0:ncols], start=(kc == 0), stop=(kc == KC - 1)),
                       r=(tn, "ones_f"), w=(P(7),), inc=True)
                act(lambda e: e.activation(out=rstd[:, 0:ncols], in_=ps[7][:, 0:ncols], func=AF.Sqrt, bias=epsb[:, 0:1], scale=1.0 / D),
                    r=(P(7), "epsb"), w=("rstd",))
                dve(lambda e: e.reciprocal(out=rstd[:, 0:ncols], in_=rstd[:, 0:ncols]), r=("rstd",), w=("rstd",))

            def norm_mod(Ax, axn, shift_m, ncols, seq_src, outT):
                rms_rstd(ncols)
                for kc in range(KC):
                    t = tmpB[kc % 2]
                    tn = ("tmpB", kc % 2)
                    dve(lambda e, kc=kc, t=t: e.tensor_tensor(out=t[:, 0:ncols], in0=xT[:, kc, 0:ncols], in1=rstd[:, 0:ncols], op=ALU.mult),
                        r=("xT", "rstd"), w=(tn,))
                    act(lambda e, kc=kc, t=t: e.activation(out=outT[:, kc, 0:ncols], in_=t[:, 0:ncols], func=AF.Identity,
                                                           bias=modT[:, shift_m, 0, kc:kc + 1], scale=Ax[:, 0, kc:kc + 1]),
                        r=(tn, "modT", axn), w=("hT",))
                    if seq_src is not None:
                        act(lambda e, kc=kc, t=t: e.activation(out=outT[:, kc, SC:SC + 4], in_=t[:, SC:SC + 4], func=AF.Identity,
                                                               bias=modT[:, shift_m, seq_src, kc:kc + 1], scale=Ax[:, seq_src, kc:kc + 1]),
                            r=(tn, "modT", axn), w=("hT",))

            def load_tables(cs_src_fn, csi_src_fn, rows_list):
                for t, rows in enumerate(rows_list):
                    dma_sp(cs_t[0:rows, t, :, :], cs_src_fn(t), w=("cs_t",))
                    dma_sp(csi_t[0:rows, t, :, :], csi_src_fn(t), w=("csi_t",))

            def rope(psrc, rows, tab, t, half, nh, dst, dstn, rd):
                xv = psrc.rearrange("p (h two d) -> p h two d", h=nh, two=2)
                dv = dst.rearrange("p (h two d) -> p h two d", h=nh, two=2)
                cosv = tab[0:rows, t, 0, 0:nh * half].rearrange("p (h d) -> p h d", h=nh)
                sinv = tab[0:rows, t, 1, 0:nh * half].rearrange("p (h d) -> p h d", h=nh)
                a, b2, c, d2 = (rp[i][0:rows, 0:nh * half].rearrange("p (h d) -> p h d", h=nh) for i in range(4))
                tabn = "cs_t" if tab is cs_t else "csi_t"
                dve(lambda e: e.tensor_tensor(out=a, in0=xv[:, :, 0, :], in1=cosv, op=ALU.mult), r=tuple(rd) + (tabn,), w=(("rp", 0),))
                dve(lambda e: e.tensor_tensor(out=b2, in0=xv[:, :, 1, :], in1=sinv, op=ALU.mult), r=tuple(rd) + (tabn,), w=(("rp", 1),))
                dve(lambda e: e.tensor_tensor(out=c, in0=xv[:, :, 1, :], in1=cosv, op=ALU.mult), r=tuple(rd) + (tabn,), w=(("rp", 2),))
                dve(lambda e: e.tensor_tensor(out=d2, in0=xv[:, :, 0, :], in1=sinv, op=ALU.mult), r=tuple(rd) + (tabn,), w=(("rp", 3),))
                dve(lambda e: e.tensor_tensor(out=dv[:, :, 0, :], in0=a, in1=b2, op=ALU.subtract), r=(("rp", 0), ("rp", 1)), w=(dstn,))
                dve(lambda e: e.tensor_tensor(out=dv[:, :, 1, :], in0=c, in1=d2, op=ALU.add), r=(("rp", 2), ("rp", 3)), w=(dstn,))

            def tok_matmul(slotname, tiles, bank0):
                s, wv, n = load_slot(slotname)
                width = lay[slotname][2]
                res = []
                for ti, (c0, rows) in enumerate(tiles):
                    b = bank0 + ti
                    for kc in range(KC):
                        pe(lambda e, kc=kc, b=b, c0=c0, rows=rows, wv=wv, width=width: e.matmul(
                            ps[b][0:rows, 0:width], lhsT=hT[:, kc, c0:c0 + rows], rhs=wv[:, kc, :], start=(kc == 0), stop=(kc == KC - 1)),
                           r=(("wr", s), "hT"), w=(P(b),), inc=(kc == KC - 1))
                    res.append((b, rows))
                return res

            def to_featmajor(src_bf, rows, dst, dstn, bank, rd):
                pe(lambda e: e.transpose(out=psb(bank)[:, 0:rows], in_=src_bf, identity=identb[0:rows, 0:rows]),
                   r=tuple(rd) + ("identb",), w=(P(bank),))
                act(lambda e: e.copy(out=dst, in_=psb(bank)[:, 0:rows]), r=(P(bank),), w=(dstn, "actT") if dstn == "qT" else (dstn,))

            ring_kv = [0]

            def kv_stage(tiles, key_cols, vtile_idx, with_q, sample_tile, out_rows):
                nt = len(tiles)
                if with_q:
                    for h in range(16):
                        res = tok_matmul(f"C_q_{h}", tiles, 0)
                        for ti, (b, rows) in enumerate(res):
                            rope(ps[b][0:rows, 0:128], rows, cs_t, ti, 64, 1, small[0:rows, 0:128], ("xst", 1), (P(b),))
                            dve(lambda e, rows=rows: e.tensor_copy(out=tokb[0:rows, :], in_=small[0:rows, 0:128]), r=(("xst", 1),), w=("tokb",))
                            c0 = tiles[ti][0]
                            to_featmajor(tokb[0:rows, :], rows, qT[:, h, c0:c0 + rows], "qT", 3 + ti % 2, ("tokb",))
                    for bq in range(8):
                        res = tok_matmul(f"C_qi_{bq}", tiles, 0)
                        for ti, (b, rows) in enumerate(res):
                            rope(ps[b][0:rows, 0:128], rows, csi_t, ti, 32, 2, small[0:rows, 0:128], ("xst", 1), (P(b),))
                            dve(lambda e, rows=rows: e.tensor_copy(out=tokb[0:rows, :], in_=small[0:rows, 0:128]), r=(("xst", 1),), w=("tokb",))
                            c0 = tiles[ti][0]
                            to_featmajor(tokb[0:rows, :], rows, qiT[:, bq, c0:c0 + rows], "qiT", 3 + ti % 2, ("tokb",))
                    res = tok_matmul("C_wi", tiles, 0)
                    for ti, (b, rows) in enumerate(res):
                        dve(lambda e, b=b, rows=rows, ti=ti: e.tensor_scalar(out=w_tok[0:rows, ti, :], in0=ps[b][0:rows, 0:16], scalar1=1.0 / 32.0, scalar2=None, op0=ALU.mult),
                            r=(P(b),), w=("w_tok",))
                for g4 in range(4):
                    res = tok_matmul(f"C_k_{g4}", tiles, 0)
                    for ti, (b, rows) in enumerate(res):
                        ri = ring_kv[0] % 2
                        ring_kv[0] += 1
                        rope(ps[b][0:rows, 0:128], rows, cs_t, ti, 64, 1, kst[0:rows, ri, :], ("kst", ri), (P(b),))
                        dve(lambda e, rows=rows, ri=ri: e.tensor_copy(out=tokb[0:rows, :], in_=kst[0:rows, ri, :]), r=(("kst", ri),), w=("tokb",))
                        if out_rows is not None:
                            dr, nr = out_rows[ti]
                            if os.environ.get("MK_O", "all") in ("k", "all") and (nr == 128 or os.environ.get("MK_O2", "s") == "s"):
                                dma_out(dr[0][g4], kst[0:nr, ri, :], r=(("kst", ri),))
                        if ti == sample_tile:
                            to_featmajor(tokb[0:4, :], 4, ksT[:, g4, :], "ksT", 3 + ti % 2, ("tokb",))
                        else:
                            kc0 = key_cols[ti]
                            to_featmajor(tokb[0:rows, :], rows, kT[:, g4, kc0:kc0 + rows], "kT", 3 + ti % 2, ("tokb",))
                for g4 in range(4):
                    res = tok_matmul(f"C_v_{g4}", tiles, 0)
                    for ti, (b, rows) in enumerate(res):
                        ri = ring_kv[0] % 2
                        ring_kv[0] += 1
                        if out_rows is not None:
                            act(lambda e, b=b, rows=rows, ri=ri: e.copy(out=vst[0:rows, ri, :], in_=ps[b][0:rows, 0:128]), r=(P(b),), w=(("vst", ri),))
                            dr, nr = out_rows[ti]
                            if os.environ.get("MK_O", "all") in ("v", "all") and (nr == 128 or os.environ.get("MK_O2", "s") == "s"):
                                dma_out(dr[1][g4], vst[0:nr, ri, :], r=(("vst", ri),))
                        if out_rows is not None:
                            vsrc, vr = vst[0:rows, ri, :], (("vst", ri),)
                        else:
                            vsrc, vr = ps[b][0:rows, 0:128], (P(b),)
                        if ti == sample_tile:
                            dve(lambda e, g4=g4, vsrc=vsrc: e.tensor_copy(out=vs_sb[0:4, g4 * 128:(g4 + 1) * 128], in_=vsrc[0:4, :]), r=vr, w=("vs_sb",))
                        else:
                            vi = vtile_idx[ti]
                            dve(lambda e, rows=rows, vi=vi, g4=g4, vsrc=vsrc: e.tensor_copy(out=v_sb[0:rows, vi, g4 * 128:(g4 + 1) * 128], in_=vsrc), r=vr, w=("v_sb",))
                res = tok_matmul("C_ki", tiles, 0)
                for ti, (b, rows) in enumerate(res):
                    rope(ps[b][0:rows, 0:128], rows, csi_t, ti, 32, 2, small[0:rows, 0:128], ("xst", 1), (P(b),))
                    if out_rows is not None:
                        dr, nr = out_rows[ti]
                        if os.environ.get("MK_O", "all") in ("ki", "all"):
                            dma_out(dr[2], small[0:nr, 0:64], r=(("xst", 1),))
                    dve(lambda e, rows=rows: e.tensor_copy(out=tokb[0:rows, :], in_=small[0:rows, 0:128]), r=(("xst", 1),), w=("tokb",))
                    if ti == sample_tile:
                        to_featmajor(tokb[0:4, :], 4, kisT2[:, :], "kisT2", 3 + ti % 2, ("tokb",))
                    else:
                        kc0 = key_cols[ti]
                        to_featmajor(tokb[0:rows, :], rows, kiT2[:, kc0:kc0 + rows], "kiT2", 3 + ti % 2, ("tokb",))


            SCALE = 128.0 ** -0.5
            attnT = vcT

            def prompt_attention(g, qt):
                c0 = qt * 128
                q0 = g * 256 + qt * 128
                NK = q0 // 128 + 1
                NKC = NK * 128
                nch = (NKC + 511) // 512
                for c in range(nch):
                    cw = min(512, NKC - c * 512)
                    for h in range(16):
                        par, blk = h % 2, h // 2
                        b = h % 2
                        rt = rtmp2[h % 2]
                        rtn = ("rtmp", h % 2)
                        pe(lambda e, b=b, cw=cw, par=par, blk=blk, c=c: e.matmul(
                            ps[b][:, 0:cw], lhsT=qiT[par * 64:(par + 1) * 64, blk, c0:c0 + 128], rhs=kiT2[par * 64:(par + 1) * 64, c * 512:c * 512 + cw], start=True, stop=True),
                           r=("qiT", "kiT2"), w=(P(b),))
                        act(lambda e, b=b, cw=cw, rt=rt: e.activation(out=rt[:, 0:cw], in_=ps[b][:, 0:cw], func=AF.Relu), r=(P(b),), w=(rtn,))
                        if h == 0:
                            dve(lambda e, cw=cw, rt=rt, c=c: e.tensor_scalar(out=sc[:, c * 512:c * 512 + cw], in0=rt[:, 0:cw], scalar1=w_tok[:, qt, 0:1], scalar2=None, op0=ALU.mult),
                                r=(rtn, "w_tok"), w=("sc",))
                        else:
                            dve(lambda e, cw=cw, rt=rt, c=c, h=h: e.scalar_tensor_tensor(out=sc[:, c * 512:c * 512 + cw], in0=rt[:, 0:cw], scalar=w_tok[:, qt, h:h + 1], in1=sc[:, c * 512:c * 512 + cw], op0=ALU.mult, op1=ALU.add),
                                r=(rtn, "w_tok", "sc"), w=("sc",))
                dve(lambda e: e.tensor_tensor(out=sc[:, NKC - 128:NKC], in0=sc[:, NKC - 128:NKC], in1=tri[:, :], op=ALU.add), r=("sc", "tri"), w=("sc",))
                dve(lambda e: e.memset(bis[:, 0:1], BIS_LO), w=("bis",))
                for k in range(NBIS):
                    cst = (-2.0 * BIS_LO) / (2.0 ** (k + 1))
                    dve(lambda e, cst=cst: e.tensor_scalar(out=bis[:, 1:2], in0=bis[:, 0:1], scalar1=cst, scalar2=None, op0=ALU.add), r=("bis",), w=("bis",))
                    dve(lambda e: e.tensor_scalar(out=mb[:, 0:NKC], in0=sc[:, 0:NKC], scalar1=bis[:, 1:2], scalar2=None, op0=ALU.is_ge, op1=ALU.add, accum_out=bis[:, 2:3]),
                        r=("sc", "bis"), w=("mb", "bis"))
                    dve(lambda e, cst=cst: e.tensor_scalar(out=bis[:, 3:4], in0=bis[:, 2:3], scalar1=255.5, scalar2=cst, op0=ALU.is_ge, op1=ALU.mult), r=("bis",), w=("bis",))
                    dve(lambda e: e.tensor_tensor(out=bis[:, 0:1], in0=bis[:, 0:1], in1=bis[:, 3:4], op=ALU.add), r=("bis",), w=("bis",))
                dve(lambda e: e.tensor_scalar(out=mb[:, 0:NKC], in0=sc[:, 0:NKC], scalar1=bis[:, 0:1], scalar2=MBIAS, op0=ALU.is_lt, op1=ALU.mult), r=("sc", "bis"), w=("mb",))
                for gg in range(4):
                    def s_mm(kt, gg=gg):
                        sb_ = 2 + kt % 2
                        pe(lambda e, sb_=sb_, kt=kt, gg=gg: e.matmul(ps[sb_][:, :].rearrange("p (h t) -> p h t", h=4), lhsT=kT[:, gg, kt * 128:(kt + 1) * 128],
                                                             rhs=qT[:, 4 * gg:4 * gg + 4, c0:c0 + 128], start=True, stop=False),
                           r=("kT", "qT"), w=(P(sb_),), inc=False)
                        pe(lambda e, sb_=sb_, kt=kt: e.matmul(ps[sb_][:, :].rearrange("p (h t) -> p h t", h=4), lhsT=mb[:, kt * 128:(kt + 1) * 128],
                                                             rhs=ident4[:, :, :], start=False, stop=True),
                           r=("mb", "ident4"), w=(P(sb_),))
                    s_mm(0)
                    for kt in range(NK):
                        if kt + 1 < NK:
                            s_mm(kt + 1)
                        sb_ = 2 + kt % 2
                        pt_ = pT[kt % 2]
                        ptn = ("pT", kt % 2)
                        act(lambda e, sb_=sb_, pt_=pt_: e.activation(out=pt_[:, :], in_=ps[sb_][:, :], func=AF.Exp, scale=SCALE), r=(P(sb_),), w=(ptn,))
                        pe(lambda e, kt=kt, pt_=pt_, gg=gg: e.matmul(ps[4][:, :], lhsT=v_sb[:, kt, gg * 128:(gg + 1) * 128], rhs=pt_[:, :], start=(kt == 0), stop=(kt == NK - 1)),
                           r=("v_sb", ptn), w=(P(4),), inc=False)
                        pe(lambda e, kt=kt, pt_=pt_: e.matmul(ps[5][:, :], lhsT=ones_b[:, :], rhs=pt_[:, :], start=(kt == 0), stop=(kt == NK - 1)),
                           r=("ones_b", ptn), w=(P(5),))
                    rd = rtmp2[0]
                    dve(lambda e: e.reciprocal(out=rd[:, :], in_=ps[5][:, :]), r=(P(5),), w=(("rtmp", 0),))
                    dve(lambda e, gg=gg: e.tensor_tensor(out=attnT[:, 4 * gg:4 * gg + 4, c0:c0 + 128], in0=ps[4][:, :].rearrange("p (h t) -> p h t", h=4),
                                                        in1=rd[:, :].rearrange("p (h t) -> p h t", h=4), op=ALU.mult),
                        r=(P(4), ("rtmp", 0)), w=("vcT",))

            def pool_op(fn, r=(), w=()):
                S.op("pool", fn, r, w)

            def sample_attention(g):
                if int(os.environ.get('MK_SA', '9')) < 1:
                    return
                for t4 in range(4):
                    dve(lambda e, t4=t4: e.tensor_scalar(out=wd[0:4, :].rearrange("p (par blk t) -> p par blk t", par=2, blk=8)[:, :, :, t4],
                                                        in0=w_tok[0:4, 2, :].rearrange("p (blk par) -> p par blk", par=2),
                                                        scalar1=ident[0:4, t4:t4 + 1], scalar2=None, op0=ALU.mult),
                        r=("w_tok", "ident"), w=("wd",))
                pe(lambda e: e.matmul(ps[6][:, 0:64], lhsT=ones_f[0:4, :], rhs=wd[0:4, :], start=True, stop=True), r=("wd", "ones_f"), w=(P(6),))
                for i in range(8):
                    dve(lambda e, i=i: e.tensor_copy(out=wbc8[:, i, :], in_=ps[6][:, 0:64]), r=(P(6),), w=("wbc8",))
                if int(os.environ.get('MK_SA', '9')) < 2:
                    return
                dve(lambda e: e.memset(scT[:, :, 64:65], NEG), w=("scT",))
                _sc = int(os.environ.get("MK_SC", "9"))
                if _sc < 1:
                    return
                for c8 in range(8):
                    for half in range(2):
                        for i in range(4):
                            j = c8 * 8 + half * 4 + i
                            kb = j % 2
                            if _sc < 2 and j >= int(os.environ.get("MK_NJ", "64")):
                                continue
                            dma_sp(kip[:, kb, 0:64], scr_ki[g, j], r=(("scr", g),), w=(("kip", kb),))
                            if _sc < 2:
                                continue
                            if os.environ.get("MK_DUP", "dve") == "dve":
                                dve(lambda e, kb=kb: e.tensor_copy(out=kip[:, kb, 64:128], in_=kip[:, kb, 0:64]), r=(("kip", kb),), w=(("kip", kb),))
                            else:
                                S.dma("pool", lambda e, j=j, kb=kb: e.indirect_dma_start(out=kip[:, kb, 64:128], out_offset=None, in_=cache_ki,
                                                                                         in_offset=bass.IndirectOffsetOnAxis(ap=idxi[:, j:j + 1], axis=0)),
                                      r=("idxi",), w=(("kip2", kb),))
                            if _sc < 3:
                                continue
                            pe(lambda e, kb=kb, i=i: e.transpose(out=psb(6)[:, i * 128:(i + 1) * 128], in_=kip[:, kb, :], identity=identb[:, :]),
                               r=(("kip", kb), ("kip2", kb), "identb"), w=(P(6),), inc=True)
                        if _sc < 4:
                            continue
                        act(lambda e: e.copy(out=kipT[:, :], in_=psb(6)[:, 0:512]), r=(P(6),), w=("kipT",))
                        for i in range(4 if os.environ.get("MK_SB", "9") >= "2" else 0):
                            pg = half * 4 + i
                            for par in range(2):
                                pe(lambda e, i=i, pg=pg, par=par: e.matmul(
                                    ps[par][:, pg * 32:(pg + 1) * 32].rearrange("p (blk t) -> p blk t", blk=8),
                                    lhsT=kipT[par * 64:(par + 1) * 64, i * 128:(i + 1) * 128], rhs=qiT[par * 64:(par + 1) * 64, :, SC:SC + 4], start=True, stop=True),
                                   r=("kipT", "qiT"), w=(P(par),), inc=(i == 3))
                    rt = rtmp2[c8 % 2]
                    rtn = ("rtmp", c8 % 2)
                    if os.environ.get("MK_SB", "9") < "3":
                        continue
                    for par in range(2):
                        act(lambda e, rt=rt, par=par: e.activation(out=rt[:, par * 256:(par + 1) * 256], in_=ps[par][:, 0:256], func=AF.Relu), r=(P(par),), w=(rtn,))
                    dve(lambda e, rt=rt: e.tensor_tensor(out=rt[:, :].rearrange("p (par i c) -> p par i c", par=2, i=8),
                                                        in0=rt[:, :].rearrange("p (par i c) -> p par i c", par=2, i=8),
                                                        in1=wbc8[:, :, :].rearrange("p i (par c) -> p par i c", par=2), op=ALU.mult), r=(rtn, "wbc8"), w=(rtn,))
                    dve(lambda e, rt=rt, c8=c8: e.tensor_reduce(out=scT[:, :, c8 * 8:(c8 + 1) * 8].rearrange("p t g -> p g t"),
                                                              in_=rt[:, 0:256].rearrange("p (g blk t) -> p g t blk", g=8, blk=8), axis=AX.X, op=ALU.add),
                        r=(rtn,), w=("scT",))
                    dve(lambda e, rt=rt: e.tensor_reduce(out=red1[:, 0:32].rearrange("p (g t) -> p g t", g=8),
                                                        in_=rt[:, 256:512].rearrange("p (g blk t) -> p g t blk", g=8, blk=8), axis=AX.X, op=ALU.add),
                        r=(rtn,), w=("red1",))
                    dve(lambda e, c8=c8: e.tensor_tensor(out=scT[:, :, c8 * 8:(c8 + 1) * 8].rearrange("p t g -> p g t"),
                                                        in0=scT[:, :, c8 * 8:(c8 + 1) * 8].rearrange("p t g -> p g t"),
                                                        in1=red1[:, 0:32].rearrange("p (g t) -> p g t", g=8), op=ALU.add),
                        r=("scT", "red1"), w=("scT",))
                for par in range(2):
                    pe(lambda e, par=par: e.matmul(ps[par][0:4, 0:32].rearrange("p (blk t) -> p blk t", blk=8),
                                                  lhsT=kisT2[par * 64:(par + 1) * 64, 0:4], rhs=qiT[par * 64:(par + 1) * 64, :, SC:SC + 4], start=True, stop=True),
                       r=("kisT2", "qiT"), w=(P(par),), inc=True)
                rt = rtmp2[0]
                rtn = ("rtmp", 0)
                for par in range(2):
                    act(lambda e, par=par: e.activation(out=rt[0:4, par * 32:(par + 1) * 32], in_=ps[par][0:4, 0:32], func=AF.Relu), r=(P(par),), w=(rtn,))
                dve(lambda e: e.tensor_tensor(out=rt[0:4, 0:64], in0=rt[0:4, 0:64], in1=wbc8[0:4, 0, :], op=ALU.mult), r=(rtn, "wbc8"), w=(rtn,))
                dve(lambda e: e.tensor_reduce(out=rt[0:4, 64:68], in_=rt[0:4, 0:64].rearrange("p (h t) -> p t h", h=16), axis=AX.X, op=ALU.add), r=(rtn,), w=(rtn,))
                dve(lambda e: e.tensor_tensor(out=scT[0:4, :, 64], in0=rt[0:4, 64:68], in1=triT[0:4, 0:4], op=ALU.add), r=(rtn, "triT"), w=("scT",))
                if int(os.environ.get('MK_SA', '9')) < 3:
                    return
                dve(lambda e: e.memset(bisS[:, 0:4], BIS_LO), w=("bisS",))
                for k in range(NBIS):
                    cst = (-2.0 * BIS_LO) / (2.0 ** (k + 1))
                    dve(lambda e, cst=cst: e.tensor_scalar(out=bisS[:, 4:8], in0=bisS[:, 0:4], scalar1=cst, scalar2=None, op0=ALU.add), r=("bisS",), w=("bisS",))
                    for t in range(4):
                        dve(lambda e, t=t: e.tensor_scalar(out=mbT4[:, :, t], in0=scT[:, t, :], scalar1=bisS[:, 4 + t:5 + t], scalar2=None, op0=ALU.is_ge, op1=ALU.add, accum_out=bisS[:, 8 + t:9 + t]),
                            r=("scT", "bisS"), w=("mbT4", "bisS"))
                    pe(lambda e: e.matmul(ps[1][:, 0:4], lhsT=ones_f[:, :], rhs=bisS[:, 8:12], start=True, stop=True), r=("bisS", "ones_f"), w=(P(1),))
                    dve(lambda e, cst=cst: e.tensor_scalar(out=bisS[:, 12:16], in0=ps[1][:, 0:4], scalar1=255.5, scalar2=cst, op0=ALU.is_ge, op1=ALU.mult), r=(P(1),), w=("bisS",))
                    dve(lambda e: e.tensor_tensor(out=bisS[:, 0:4], in0=bisS[:, 0:4], in1=bisS[:, 12:16], op=ALU.add), r=("bisS",), w=("bisS",))
                for t in range(4):
                    for hh in range(4):
                        dve(lambda e, t=t, hh=hh: e.tensor_scalar(out=mbT4[:, :, hh * 4 + t], in0=scT[:, t, :], scalar1=bisS[:, t:t + 1], scalar2=MBIAS, op0=ALU.is_lt, op1=ALU.mult),
                            r=("scT", "bisS"), w=("mbT4",))
                if int(os.environ.get('MK_SA', '9')) < 4:
                    return
                pe(lambda e: e.matmul(ps[4][:, 0:64], lhsT=zeros_b[:, :], rhs=ones_b[:, 0:64], start=True, stop=True), r=("zeros_b", "ones_b"), w=(P(4),), inc=False)
                for j in range(NPAGE + 1):
                    kb = j % 2
                    last = (j == NPAGE)
                    if not last:
                        dma_sp(kpg[:, kb, :], scr_k[g, j], r=(("scr", g),), w=(("kpg", kb),))
                        dma_sp(vpg[:, kb, :], scr_v[g, j], r=(("scr", g),), w=(("vpg", kb),))
                        for gg in range(4):
                            pe(lambda e, kb=kb, gg=gg: e.transpose(out=psb(6)[:, gg * 128:(gg + 1) * 128], in_=kpg[:, kb, gg * 128:(gg + 1) * 128], identity=identb[:, :]),
                               r=(("kpg", kb), "identb"), w=(P(6),), inc=(gg == 3))
                        act(lambda e, kb=kb: e.copy(out=kpT[:, 0, :], in_=psb(6)[:, 0:512]), r=(P(6),), w=(("kpT", 0),))
                    nk = 4 if last else 128
                    sb_ = 2 + j % 2
                    for gg in range(4):
                        kl = ksT[:, gg, 0:4] if last else kpT[:, 0, gg * 128:(gg + 1) * 128]
                        pe(lambda e, sb_=sb_, gg=gg, kl=kl, nk=nk: e.matmul(ps[sb_][0:nk, gg * 16:(gg + 1) * 16].rearrange("p (h t) -> p h t", h=4), lhsT=kl,
                                                                           rhs=qT[:, 4 * gg:4 * gg + 4, SC:SC + 4], start=True, stop=False),
                           r=(("kpT", 0), "ksT", "qT"), w=(P(sb_),), inc=False)
                        pe(lambda e, sb_=sb_, gg=gg, j=j, nk=nk: e.matmul(ps[sb_][0:nk, gg * 16:(gg + 1) * 16], lhsT=identb[0:nk, 0:nk], rhs=mbT4[0:nk, j, :], start=False, stop=True),
                           r=("identb", "mbT4"), w=(P(sb_),), inc=(gg == 3))
                    act(lambda e, sb_=sb_, kb=kb, nk=nk: e.activation(out=pTs[0:nk, kb, :], in_=ps[sb_][0:nk, 0:64], func=AF.Exp, scale=SCALE), r=(P(sb_),), w=(("pTs", kb),))
                    for gg in range(4):
                        vl = vs_sb[0:4, gg * 128:(gg + 1) * 128] if last else vpg[:, kb, gg * 128:(gg + 1) * 128]
                        pe(lambda e, gg=gg, vl=vl, kb=kb, nk=nk: e.matmul(ps[4][:, gg * 16:(gg + 1) * 16], lhsT=vl, rhs=pTs[0:nk, kb, gg * 16:(gg + 1) * 16], start=False, stop=True, skip_group_check=True),
                           r=(("vpg", kb), "vs_sb", ("pTs", kb)), w=(P(4),), inc=False)
                    pe(lambda e, kb=kb, nk=nk, j=j: e.matmul(ps[5][:, 0:64], lhsT=ones_b[0:nk, :], rhs=pTs[0:nk, kb, :], start=(j == 0), stop=(j == NPAGE)),
                       r=("ones_b", ("pTs", kb)), w=(P(5),))
                rd = rtmp2[1]
                dve(lambda e: e.reciprocal(out=rd[:, 0:64], in_=ps[5][:, 0:64]), r=(P(5),), w=(("rtmp", 1),))
                dve(lambda e: e.tensor_tensor(out=attnT[:, :, SC:SC + 4], in0=ps[4][:, 0:64].rearrange("p (h t) -> p h t", t=4),
                                              in1=rd[:, 0:64].rearrange("p (h t) -> p h t", t=4), op=ALU.mult),
                    r=(P(4), ("rtmp", 1)), w=("vcT",))

            NO_ATTN = os.environ.get("MK_NOATTN", "0")
            def ATTN_HOOK(g, gi):
                if NO_ATTN == "1":
                    return
                for qt in range(2):
                    prompt_attention(g, qt)
                if NO_ATTN != "2":
                    sample_attention(gi)

            def SAMPLE_Y(g):
                pass

            CHECK(3)
            CHECK(4)
            for gi in range(int(os.environ.get("MK_NG", NG))):
                sidx, g = gi // GPS, gi % GPS
                tok0 = sidx * 2048 + g * 256
                if g == 0:
                    adaln(sidx)
                seq_src = 1 + g
                dma_sp(small[0:32, 0:128], sconv[gi], w=(("xst", 1),))
                transpose_rows(small[0:32, 0:128], 32, scvT[:, 0, :], 0, (("xst", 1),), ("scvT",))
                def xrows(t, g=g, tok0=tok0, gi=gi):
                    if t < 2:
                        return x_own[tok0 + t * 128: tok0 + (t + 1) * 128, :]
                    return x_small[gi]
                load_x_tiles(xrows, [128, 128, 6])
                norm_mod(A1, "A1", 0, G, seq_src, hT)

                CHECK(5)
                for cc in range(16):
                    banks = []
                    for i, nm in enumerate(("xin", "bg", "cg")):
                        s, wv, n = load_slot(f"A_{nm}_{cc}")
                        b = (cc % 2) * 3 + i
                        banks.append(b)
                        for kc in range(KC):
                            pe(lambda e, kc=kc, b=b, wv=wv: e.matmul(ps[b][:, 0:G], lhsT=wv[:, kc, :], rhs=hT[:, kc, :], start=(kc == 0), stop=(kc == KC - 1)),
                               r=(("wr", s), "hT"), w=(P(b),), inc=(kc == KC - 1))
                    bx, bb, bc = banks
                    cgs = tmpA[cc % 2]
                    cgn = ("tmpA", cc % 2)
                    act(lambda e, bc=bc, cgs=cgs: e.copy(out=cgs[:, 0:G], in_=ps[bc][:, 0:G]), r=(P(bc),), w=(cgn,))
                    dve(lambda e, bx=bx, cgs=cgs: e.tensor_tensor(out=aext[:, 2:2 + GP], in0=ps[bx][:, 0:GP], in1=cgs[:, 0:GP], op=ALU.mult), r=(P(bx), cgn), w=("aext",))
                    dve(lambda e, bx=bx, cgs=cgs: e.tensor_tensor(out=aext[:, 0:2], in0=ps[bx][:, HC:HC + 2], in1=cgs[:, HC:HC + 2], op=ALU.mult), r=(P(bx), cgn), w=("aext",))
                    if g == 0:
                        dve(lambda e: e.tensor_scalar(out=aext[:, 0:2], in0=aext[:, 0:2], scalar1=flg[:, 0:1], scalar2=None, op0=ALU.mult), r=("aext", "flg"), w=("aext",))
                    dve(lambda e, bx=bx, cgs=cgs: e.tensor_tensor(out=aexs[:, 2:6], in0=ps[bx][:, SC:SC + 4], in1=cgs[:, SC:SC + 4], op=ALU.mult), r=(P(bx), cgn), w=("aexs",))
                    dve(lambda e, cc=cc: e.tensor_copy(out=aexs[:, 0:2], in_=scvT[:, 0, cc:cc + 17:16]), r=("scvT",), w=("aexs",))
                    for (ax, axn, T, ub, ubn) in ((aext, "aext", GP, ubuf, "ubuf"), (aexs, "aexs", 4, ubs, "ubs")):
                        dve(lambda e, ax=ax, T=T, ub=ub, cc=cc: e.tensor_scalar(out=ub[:, 0:T], in0=ax[:, 0:T], scalar1=cwT[:, cc:cc + 1], scalar2=None, op0=ALU.mult), r=(axn, "cwT"), w=(ubn,))
                        dve(lambda e, ax=ax, T=T, ub=ub, cc=cc: e.scalar_tensor_tensor(out=ub[:, 0:T], in0=ax[:, 1:T + 1], scalar=cwT[:, 16 + cc:17 + cc], in1=ub[:, 0:T], op0=ALU.mult, op1=ALU.add), r=(axn, "cwT", ubn), w=(ubn,))
                        dve(lambda e, ax=ax, T=T, ub=ub, cc=cc: e.scalar_tensor_tensor(out=ub[:, 0:T], in0=ax[:, 2:T + 2], scalar=cwT[:, 32 + cc:33 + cc], in1=ub[:, 0:T], op0=ALU.mult, op1=ALU.add), r=(axn, "cwT", ubn), w=(ubn,))
                    dve(lambda e, bb=bb, cc=cc: e.tensor_tensor(out=vcT[:, cc, 0:GP], in0=ps[bb][:, 0:GP], in1=ubuf[:, 0:GP], op=ALU.mult), r=(P(bb), "ubuf"), w=("vcT", "actT"))
                    dve(lambda e, bb=bb, cc=cc: e.tensor_tensor(out=vcT[:, cc, SC:SC + 4], in0=ps[bb][:, SC:SC + 4], in1=ubs[:, 0:4], op=ALU.mult), r=(P(bb), "ubs"), w=("vcT", "actT"))
                    dve(lambda e, cc=cc: e.tensor_copy(out=alast_p[:, cc:cc + 17:16], in_=aext[:, GP:GP + 2]), r=("aext",), w=("alast_p",))
                    dve(lambda e, cc=cc: e.tensor_copy(out=alast_s[:, cc:cc + 17:16], in_=aexs[:, 4:6]), r=("aexs",), w=("alast_s",))
                pe(lambda e: e.transpose(out=ps[6][0:32, 0:128], in_=alast_s[:, :], identity=ident[:, :]), r=("alast_s", "ident"), w=(P(6),))
                dve(lambda e: e.tensor_copy(out=ost[:, :], in_=ps[6][0:32, 0:128]), r=(P(6),), w=("ost",))
                dma_sp(conv_s[gi], ost[:, :], r=("ost",))
                if os.environ.get("MK_T") == "1":
                    dma_sp(ki_own[0:128, :], kist[0:128, 0, :], r=())
                if os.environ.get("MK_T") == "2":
                    dma_sp(ki_own[0:32, :], ost[:, 0:64], r=("ost",))
                if g == GPS - 1:
                    pe(lambda e: e.transpose(out=ps[6][0:32, 0:128], in_=alast_p[:, :], identity=ident[:, :]), r=("alast_p", "ident"), w=(P(6),))
                    dve(lambda e: e.tensor_copy(out=ost[:, :], in_=ps[6][0:32, 0:128]), r=(P(6),), w=("ost",))
                    dma_sp(conv_p[sidx], ost[:, :], r=("ost",))

                CHECK(6)
                for j in range(32):
                    s1, wv1, _ = load_slot(f"B_gc_{j}")
                    b1 = (j % 2) * 2
                    for kc in range(KC):
                        pe(lambda e, kc=kc, b=b1, wv=wv1: e.matmul(ps[b][:, 0:G], lhsT=wv[:, kc, :], rhs=hT[:, kc, :], start=(kc == 0), stop=(kc == KC - 1)),
                           r=(("wr", s1), "hT"), w=(P(b1),), inc=(kc == KC - 1))
                    s2, wv2, _ = load_slot(f"B_co_{j}")
                    b2 = b1 + 1
                    for kc in range(16):
                        pe(lambda e, kc=kc, b=b2, wv=wv2: e.matmul(ps[b][:, 0:G], lhsT=wv[:, kc, :], rhs=(hT if os.environ.get("MK_DBG") == "coh" else vcT)[:, kc, :], start=(kc == 0), stop=(kc == 15)),
                           r=(("wr", s2), "vcT"), w=(P(b2),), inc=(kc == 15))
                    t = tmpA[j % 2]
                    tn = ("tmpA", j % 2)
                    _f = AF.Copy if os.environ.get("MK_DBG") == "nosig" else AF.Sigmoid
                    act(lambda e, b=b1, t=t, _f=_f: e.activation(out=t[:, 0:G], in_=ps[b][:, 0:G], func=_f), r=(P(b1),), w=(tn,))
                    if os.environ.get("MK_DBG") != "nodve":
                        dve(lambda e, b=b2, t=t, j=j: e.tensor_tensor(out=mT[:, j, :], in0=ps[b][:, 0:G], in1=t[:, 0:G], op=ALU.mult), r=(P(b2), tn), w=("mT",))

                CHECK(7)
                def tabs_cs(t, g=g):
                    if t < 2:
                        return cs_own[g * 256 + t * 128: g * 256 + (t + 1) * 128]
                    return cs_small
                def tabs_csi(t, g=g):
                    if t < 2:
                        return csi_own[g * 256 + t * 128: g * 256 + (t + 1) * 128]
                    return csi_small
                load_tables(tabs_cs, tabs_csi, [128, 128, 6])

                orow = []
                for t in range(2):
                    r0 = tok0 + t * 128
                    orow.append((([k_own[q, r0:r0 + 128, :] for q in range(4)], [v_own[q, r0:r0 + 128, :] for q in range(4)], ki_own[r0:r0 + 128, :]), 128))
                orow.append((([k_s[q, gi * 4:(gi + 1) * 4, :] for q in range(4)], [v_s[q, gi * 4:(gi + 1) * 4, :] for q in range(4)], ki_s[gi * 4:(gi + 1) * 4, :]), 4))
                _mc = os.environ.get("MK_C", "")
                if _mc == "nosmall":
                    kv_stage([(0, 128), (128, 128)], [g * 256, g * 256 + 128], [g * 2, g * 2 + 1], True, None, orow[0:2])
                elif _mc == "noq":
                    kv_stage([(0, 128), (128, 128), (SC, 6)], [g * 256, g * 256 + 128, None], [g * 2, g * 2 + 1, None], False, 2, orow)
                elif _mc == "noout":
                    kv_stage([(0, 128), (128, 128), (SC, 6)], [g * 256, g * 256 + 128, None], [g * 2, g * 2 + 1, None], True, 2, None)
                else:
                    kv_stage([(0, 128), (128, 128), (SC, 6)], [g * 256, g * 256 + 128, None], [g * 2, g * 2 + 1, None], True, 2, orow)

                if os.environ.get("MK_T") == "3":
                    dma_sp(conv_p, ost[:, :], r=("ost",))
                if os.environ.get("MK_T") == "4":
                    dma_sp(ki_own[0:128, :], kist[0:128, 0, :], r=())
                CHECK(8)
                ATTN_HOOK(g, gi)
                CHECK(9)

                for j in range(32):
                    s1, wv1, _ = load_slot(f"D_ga_{j}")
                    b1 = (j % 2) * 2
                    for kc in range(KC):
                        pe(lambda e, kc=kc, b=b1, wv=wv1: e.matmul(ps[b][:, 0:G], lhsT=wv[:, kc, :], rhs=hT[:, kc, :], start=(kc == 0), stop=(kc == KC - 1)),
                           r=(("wr", s1), "hT"), w=(P(b1),), inc=(kc == KC - 1))
                    s2, wv2, _ = load_slot(f"D_ao_{j}")
                    b2 = b1 + 1
                    for kc in range(16):
                        pe(lambda e, kc=kc, b=b2, wv=wv2: e.matmul(ps[b][:, 0:G], lhsT=wv[:, kc, :], rhs=attnT[:, kc, :], start=(kc == 0), stop=(kc == 15)),
                           r=(("wr", s2), "vcT"), w=(P(b2),), inc=(kc == 15))
                    t = tmpA[j % 2]
                    tn = ("tmpA", j % 2)
                    act(lambda e, b=b1, t=t: e.activation(out=t[:, 0:G], in_=ps[b][:, 0:G], func=AF.Sigmoid), r=(P(b1),), w=(tn,))
                    t2 = tmpB[j % 2]
                    tn2 = ("tmpB", j % 2)
                    dve(lambda e, b=b2, t=t, t2=t2: e.tensor_tensor(out=t2[:, 0:G], in0=ps[b][:, 0:G], in1=t[:, 0:G], op=ALU.mult), r=(P(b2), tn), w=(tn2,))
                    dve(lambda e, t2=t2, j=j: e.tensor_tensor(out=mT[:, j, :], in0=mT[:, j, :], in1=t2[:, 0:G], op=ALU.add), r=("mT", tn2), w=("mT",))

                CHECK(10)
                def resid(bank, j, gate_m, seq_src=seq_src):
                    dve(lambda e: e.scalar_tensor_tensor(out=xT[:, j, 0:SC], in0=ps[bank][:, 0:SC], scalar=modT[:, gate_m, 0, j:j + 1], in1=xT[:, j, 0:SC], op0=ALU.mult, op1=ALU.add),
                        r=(P(bank), "modT", "xT"), w=("xT",))
                    dve(lambda e: e.scalar_tensor_tensor(out=xT[:, j, SC:SC + 4], in0=ps[bank][:, SC:SC + 4], scalar=modT[:, gate_m, seq_src, j:j + 1], in1=xT[:, j, SC:SC + 4], op0=ALU.mult, op1=ALU.add),
                        r=(P(bank), "modT", "xT"), w=("xT",))
                for j in range(32):
                    s1, wv1, _ = load_slot(f"E_{j}")
                    b1 = j % 2
                    for kc in range(KC):
                        pe(lambda e, kc=kc, b=b1, wv=wv1: e.matmul(ps[b][:, 0:G], lhsT=wv[:, kc, :], rhs=mT[:, kc, :], start=(kc == 0), stop=(kc == KC - 1)),
                           r=(("wr", s1), "mT"), w=(P(b1),), inc=(kc == KC - 1))
                    resid(b1, j, 2)

                CHECK(11)
                norm_mod(A2, "A2", 3, G, seq_src, hT)

                aT = actT[0]
                for fb in range(FFB):
                    for f in range(32):
                        s1, wv1, _ = load_slot(f"F_up_{fb}_{f}")
                        b1 = f % 2
                        for kc in range(KC):
                            pe(lambda e, kc=kc, b=b1, wv=wv1: e.matmul(ps[b][:, 0:G], lhsT=wv[:, kc, :], rhs=hT[:, kc, :], start=(kc == 0), stop=(kc == KC - 1)),
                               r=(("wr", s1), "hT"), w=(P(b1),), inc=(kc == KC - 1))
                        t = tmpA[f % 2]
                        tn = ("tmpA", f % 2)
                        act(lambda e, b=b1, t=t: e.activation(out=t[:, 0:G], in_=ps[b][:, 0:G], func=AF.Relu), r=(P(b1),), w=(tn,))
                        dve(lambda e, t=t, f=f: e.tensor_tensor(out=aT[:, f, :], in0=t[:, 0:G], in1=t[:, 0:G], op=ALU.mult), r=(tn,), w=("actT", "vcT", "qT"))
                    for j in range(32):
                        s1, wv1, _ = load_slot(f"F_dn_{fb}_{j}")
                        b1 = 2 + j % 2
                        for kc in range(32):
                            pe(lambda e, kc=kc, b=b1, wv=wv1: e.matmul(ps[b][:, 0:G], lhsT=wv[:, kc, :], rhs=aT[:, kc, :], start=(kc == 0), stop=(kc == 31)),
                               r=(("wr", s1), "actT"), w=(P(b1),), inc=(kc == 31))
                        resid(b1, j, 5)

                CHECK(12)
                rms_rstd(G)
                for qd in range(4):
                    for q2 in range(2):
                        for i in range(4):
                            kc = qd * 8 + q2 * 4 + i
                            t = tmpB[kc % 2]
                            tn = ("tmpB", kc % 2)
                            dve(lambda e, kc=kc, t=t: e.tensor_tensor(out=t[:, 0:G], in0=xT[:, kc, :], in1=rstd[:, 0:G], op=ALU.mult), r=("xT", "rstd"), w=(tn,))
                            act(lambda e, kc=kc, t=t: e.activation(out=t[:, 0:G], in_=t[:, 0:G], func=AF.Identity, scale=vecT[:, 64 + kc:65 + kc]), r=(tn, "vecT"), w=(tn,))
                            for ti, (c0, rows) in enumerate(((0, 128), (128, 128), (SC, 4))):
                                b = 4 + ti
                                pe(lambda e, t=t, b=b, c0=c0, rows=rows, i=i: e.transpose(out=ps[b][0:rows, i * 128:(i + 1) * 128], in_=t[:, c0:c0 + rows], identity=ident[:, :]),
                                   r=(tn, "ident"), w=(P(b),), inc=(ti == 2 or i == 3))
                        c = q2 * 512
                        dve(lambda e, c=c: e.tensor_copy(out=xst[0][:, c:c + 512], in_=ps[4][:, :]), r=(P(4),), w=(("xst", 0),))
                        act(lambda e, c=c: e.copy(out=xst[1][:, c:c + 512], in_=ps[5][:, :]), r=(P(5),), w=(("xst", 1),))
                        dve(lambda e: e.tensor_copy(out=rtmp2[0][0:4, :], in_=ps[6][0:4, :]), r=(P(6),), w=(("rtmp", 0),))
                        dma_sp(y_s[gi * 4:(gi + 1) * 4, qd * 1024 + c:qd * 1024 + c + 512], rtmp2[0][0:4, :], r=(("rtmp", 0),))
                    for ti in range(2):
                        r0 = tok0 + ti * 128
                        dma_sp(y_own[r0:r0 + 128, qd * 1024:(qd + 1) * 1024], xst[ti][:, :], r=(("xst", ti),))


        except _Stop:
            pass
        S.final_wait_all("sp")
        S.emit(nc, stack)
    return nc


_NC_CACHE = {}


def _rope_tab(pos, half, dup):
    inv = (10000.0 ** (-np.arange(half, dtype=np.float32) / np.float32(half))).astype(np.float32)
    ang = (pos.astype(np.float32)[:, None] * inv[None, :]).astype(np.float32)
    c = np.cos(ang).astype(np.float32)
    s = np.sin(ang).astype(np.float32)
    if dup > 1:
        c = np.tile(c, (1, dup))
        s = np.tile(s, (1, dup))
    return np.ascontiguousarray(np.stack([c, s], axis=1))


def kernel(x_prompt, x_sample, cache_k, cache_v, cache_kidx, state_conv, page_table,
           c_prompt, c_sample, w_ada, b_ada, g_mix, w_in, conv_w, w_conv_out, w_attn_out,
           w_out, g_ffn, w_up, w_down, g_final):
    f32 = np.float32
    x_prompt = np.asarray(x_prompt, f32)
    x_sample = np.asarray(x_sample, f32)
    mats = {"in": np.asarray(w_in, f32)[0], "co": np.asarray(w_conv_out, f32)[0], "ao": np.asarray(w_attn_out, f32)[0],
            "out": np.asarray(w_out, f32)[0], "up": np.asarray(w_up, f32)[0], "down": np.asarray(w_down, f32)[0]}
    wflat = build_wflat(mats)
    adaflat = build_adaflat(np.asarray(w_ada, f32)[0])
    del mats
    ck = np.ascontiguousarray(np.asarray(cache_k, f32)[0]).reshape(-1, 512)
    cv = np.ascontiguousarray(np.asarray(cache_v, f32)[0]).reshape(-1, 512)
    cki = np.ascontiguousarray(np.asarray(cache_kidx, f32)[0]).reshape(-1, 64)
    ident = np.eye(128, dtype=f32)
    tri = np.where(np.arange(128)[None, :] <= np.arange(128)[:, None], 0.0, NEG).astype(f32)
    triT = np.ascontiguousarray(tri.T)
    vecs = np.concatenate([np.asarray(g_mix, f32)[0].reshape(32, 128), np.asarray(g_ffn, f32)[0].reshape(32, 128),
                           np.asarray(g_final, f32).reshape(32, 128)], axis=0)
    convw = np.ascontiguousarray(np.asarray(conv_w, f32)[0].reshape(48, 128))
    b_ada2 = np.ascontiguousarray(np.asarray(b_ada, f32)[0].reshape(192, 128))
    pidx = np.arange(128, dtype=f32).reshape(128, 1)
    pt = np.asarray(page_table, np.int32)
    sconv_all = np.asarray(state_conv, f32)[0]
    cs_small = np.zeros((6, 2, 64), f32)
    cs_small[0:4] = _rope_tab(PAST + np.arange(4), 64, 1)
    csi_small = np.zeros((6, 2, 64), f32)
    csi_small[0:4] = _rope_tab(PAST + np.arange(4), 32, 2)
    cs_own = _rope_tab(np.arange(2048), 64, 1)
    csi_own = _rope_tab(np.arange(2048), 32, 2)
    c_prompt = np.asarray(c_prompt, f32)
    c_sample = np.asarray(c_sample, f32)
    flags = np.zeros((128, 2), f32)
    in_maps = []
    for c in range(NCORES):
        x_small = np.empty((NG, 6, D), f32)
        c5 = np.empty((2, NSRC, D), f32)
        for gi in range(NG):
            sidx, g = gi // GPS, gi % GPS
            b = 2 * c + sidx
            x_small[gi, 0:4] = x_sample[16 * c + gi]
            p0 = g * 256 - 2
            x_small[gi, 4:6] = x_prompt[b, 0:2] if p0 < 0 else x_prompt[b, p0:p0 + 2]
        for sidx in range(2):
            c5[sidx, 0] = c_prompt[2 * c + sidx]
            c5[sidx, 1:9] = c_sample[16 * c + sidx * 8:16 * c + sidx * 8 + 8]
        in_maps.append({
            "x_own": np.ascontiguousarray(x_prompt[2 * c:2 * c + 2].reshape(4096, D)),
            "x_small": x_small,
            "c5": c5,
            "flags": flags,
            "cs_own": cs_own, "cs_small": cs_small, "csi_own": csi_own, "csi_small": csi_small,
            "ident": ident, "tri": tri, "triT": triT,
            "wflat": wflat, "adaflat": adaflat, "b_ada": b_ada2, "vecs": vecs, "convw": convw,
            "sconv": np.ascontiguousarray(sconv_all[16 * c:16 * c + 16].reshape(16, 32, 128)),
            "cache_k": ck, "cache_v": cv, "cache_ki": cki,
            "ptab": np.ascontiguousarray(pt[16 * c:16 * c + 16]),
            "pidx": pidx,
        })
    if os.environ.get("MK_PREP_ONLY") == "1":
        return in_maps
    if "nc" not in _NC_CACHE:
        _NC_CACHE["nc"] = build_program()
    nc = _NC_CACHE["nc"]
    res = run_bass_kernel_spmd(nc, in_maps, core_ids=list(range(NCORES)))
    R = res.results
    y_prompt = np.empty((4, 2048, D), f32)
    y_sample = np.empty((32, 4, D), f32)
    k_prompt = np.empty((1, 4, 2048, 4, 128), f32)
    v_prompt = np.empty((1, 4, 2048, 4, 128), f32)
    kidx_prompt = np.empty((1, 4, 2048, 64), f32)
    conv_prompt = np.empty((1, 4, 2, DCONV), f32)
    k_sample = np.empty((1, 32, 4, 4, 128), f32)
    v_sample = np.empty((1, 32, 4, 4, 128), f32)
    kidx_sample = np.empty((1, 32, 4, 64), f32)
    conv_sample = np.empty((1, 32, 2, DCONV), f32)
    for c in range(NCORES):
        r = R[c]
        bs = slice(2 * c, 2 * c + 2)
        ss = slice(16 * c, 16 * c + 16)
        y_prompt[bs] = r["y_own"].reshape(2, 2048, D)
        y_sample[ss] = r["y_s"].reshape(16, 4, D)
        k_prompt[0, bs] = r["k_own"].transpose(1, 0, 2).reshape(2, 2048, 4, 128)
        v_prompt[0, bs] = r["v_own"].transpose(1, 0, 2).reshape(2, 2048, 4, 128)
        kidx_prompt[0, bs] = r["ki_own"].reshape(2, 2048, 64)
        k_sample[0, ss] = r["k_s"].transpose(1, 0, 2).reshape(16, 4, 4, 128)
        v_sample[0, ss] = r["v_s"].transpose(1, 0, 2).reshape(16, 4, 4, 128)
        kidx_sample[0, ss] = r["ki_s"].reshape(16, 4, 64)
        conv_sample[0, ss] = r["conv_s"].reshape(16, 2, DCONV)
        conv_prompt[0, bs] = r["conv_p"].reshape(2, 2, DCONV)
    return (y_prompt, y_sample, k_prompt, v_prompt, kidx_prompt, conv_prompt,
            k_sample, v_sample, kidx_sample, conv_sample)
```
